# Optimizing a Trainium2 kernel written in Bass

```python
import math
import jax, jax.numpy as jnp
from jax import lax
import numpy as np

D_MODEL = 2048
BATCH = 2
SEQ = 4096
DEPTH = 4

CHUNK = 64
Q_BLOCK = 128
HEAD_DIM = 128
ROPE_THETA = 10000.0
LN_EPS = 1e-5
RMS_EPS = 1e-6
NEG_INF = -1e30

H_FOX = 8
H_CHK = 8
LEFT_CHUNKS = 8
BAND = (LEFT_CHUNKS + 1) * CHUNK
REL_MAX = 128

H_MLA = 8
Q_LORA = 512
KV_LORA = 512
MLA_NOPE = 128
MLA_ROPE = 64
MLA_V = 128
H_DIFF = 4
DIFF_DIM = 128

N_KEYS = 128
N_EXPERTS = N_KEYS * N_KEYS
PEER_HEADS = 8
PEER_TOPK = 16
PEER_DK = 256
PEER_TOKEN_BLOCK = 128

N_EVEN = (DEPTH + 1) // 2
N_ODD = DEPTH // 2
ALPHA = (2 * DEPTH) ** 0.25
BETA = (8 * DEPTH) ** -0.25

EVEN_SPLITS = (H_FOX * HEAD_DIM, H_FOX * HEAD_DIM, H_FOX * HEAD_DIM, H_FOX,
               H_CHK * HEAD_DIM, H_CHK * HEAD_DIM, H_CHK * HEAD_DIM)
ODD_SPLITS = (Q_LORA, KV_LORA, MLA_ROPE,
              H_DIFF * 2 * DIFF_DIM, H_DIFF * 2 * DIFF_DIM, H_DIFF * 2 * DIFF_DIM)
IN_EVEN = sum(EVEN_SPLITS)
IN_ODD = sum(ODD_SPLITS)
MIX_WIDTH_EVEN = (H_FOX + H_CHK) * HEAD_DIM
MIX_WIDTH_ODD = H_MLA * MLA_V + H_DIFF * 2 * DIFF_DIM

kernel_name = 'hybrid_chunk_causal_encoder'


def _split(a, sizes):
    return jnp.split(a, np.cumsum(sizes)[:-1].tolist(), axis=-1)


def layer_norm(x, g, b):
    xf = x.astype(jnp.float32)
    mu = jnp.mean(xf, axis=-1, keepdims=True)
    var = jnp.mean(jnp.square(xf - mu), axis=-1, keepdims=True)
    return ((xf - mu) * lax.rsqrt(var + LN_EPS) * g + b).astype(x.dtype)


def rms_norm(x, g):
    xf = x.astype(jnp.float32)
    return (xf * lax.rsqrt(jnp.mean(jnp.square(xf), axis=-1, keepdims=True) + RMS_EPS) * g).astype(x.dtype)


def rope(x, pos):
    d = x.shape[-1]
    inv = ROPE_THETA ** (-jnp.arange(0, d, 2, dtype=jnp.float32) / d)
    ang = pos.astype(jnp.float32)[:, None] * inv[None, :]
    cos = jnp.cos(ang)[:, None, :]
    sin = jnp.sin(ang)[:, None, :]
    xf = x.astype(jnp.float32)
    x1, x2 = xf[..., : d // 2], xf[..., d // 2:]
    return jnp.concatenate([x1 * cos - x2 * sin, x2 * cos + x1 * sin], axis=-1).astype(x.dtype)


def sweep_attention(q, k, v, scale, per_frame, log_decay=None):
    B, S, H, dk = q.shape
    nb = S // Q_BLOCK
    k_pos = jnp.arange(S)
    qb = jnp.moveaxis(q.reshape(B, nb, Q_BLOCK, H, dk), 1, 0)
    xs = (jnp.arange(nb), qb)
    if log_decay is not None:
        cum_k = jnp.transpose(log_decay, (0, 2, 1))
        xs = xs + (jnp.moveaxis(log_decay.reshape(B, nb, Q_BLOCK, H), 1, 0),)

    def block(xs_i):
        i, qi = xs_i[0], xs_i[1]
        q_pos = i * Q_BLOCK + jnp.arange(Q_BLOCK)
        s = jnp.einsum('bqhd,bkhd->bhqk', qi, k, preferred_element_type=jnp.float32) * scale
        if per_frame:
            allowed = k_pos[None, :] <= q_pos[:, None]
        else:
            allowed = (k_pos[None, :] // CHUNK) <= (q_pos[:, None] // CHUNK)
        if log_decay is not None:
            ci = jnp.transpose(xs_i[2], (0, 2, 1))
            s = s + ci[..., None] - cum_k[:, :, None, :]
        s = jnp.where(allowed, s, NEG_INF)
        p = jax.nn.softmax(s, axis=-1)
        return jnp.einsum('bhqk,bkhd->bqhd', p.astype(v.dtype), v)

    out = lax.map(block, xs)
    return jnp.moveaxis(out, 0, 1).reshape(B, S, H, v.shape[-1])


def chunk_band_attention(q, k, v, rel_bias):
    B, S, H, d = q.shape
    nc = S // CHUNK
    pad = LEFT_CHUNKS * CHUNK
    kp = jnp.pad(k, ((0, 0), (pad, 0), (0, 0), (0, 0)))
    vp = jnp.pad(v, ((0, 0), (pad, 0), (0, 0), (0, 0)))
    band_idx = jnp.arange(nc)[:, None] * CHUNK + jnp.arange(BAND)[None, :]
    kb = kp[:, band_idx]
    vb = vp[:, band_idx]
    valid = band_idx >= pad
    qc = q.reshape(B, nc, CHUNK, H, d)
    s = jnp.einsum('bcqhd,bckhd->bhcqk', qc, kb, preferred_element_type=jnp.float32) * (d ** -0.5)
    rel = (jnp.arange(CHUNK)[:, None] + pad) - jnp.arange(BAND)[None, :]
    ridx = jnp.clip(rel, -REL_MAX, REL_MAX) + REL_MAX
    bias = rel_bias.astype(jnp.float32)[:, ridx]
    s = s + bias[None, :, None, :, :]
    s = jnp.where(valid[None, None, :, None, :], s, NEG_INF)
    p = jax.nn.softmax(s, axis=-1)
    o = jnp.einsum('bhcqk,bckhd->bcqhd', p.astype(v.dtype), vb)
    return o.reshape(B, S, H, d)


def even_mixer(x, w_in, b_forget, rel_bias, w_out):
    B, S, _ = x.shape
    fq, fk, fv, f_logit, cq, ck, cv = _split(x @ w_in, EVEN_SPLITS)

    def heads(t, h):
        return t.reshape(B, S, h, HEAD_DIM)

    log_f = jax.nn.log_sigmoid((f_logit + b_forget).astype(jnp.float32))
    cum_log_f = jnp.cumsum(log_f, axis=1)
    o_fox = sweep_attention(heads(fq, H_FOX), heads(fk, H_FOX), heads(fv, H_FOX),
                            HEAD_DIM ** -0.5, True, cum_log_f)
    o_chk = chunk_band_attention(heads(cq, H_CHK), heads(ck, H_CHK), heads(cv, H_CHK), rel_bias)
    o = jnp.concatenate([o_fox.reshape(B, S, -1), o_chk.reshape(B, S, -1)], axis=-1)
    return o @ w_out


def odd_mixer(x, w_in, g_q_lora, g_kv_lora, w_uq, w_ukv, diff_lambda, g_subln, w_out, pos, layer_idx):
    B, S, _ = x.shape
    cq, ckv, kpe, dq, dk, dv = _split(x @ w_in, ODD_SPLITS)

    q = (rms_norm(cq, g_q_lora) @ w_uq).reshape(B, S, H_MLA, MLA_NOPE + MLA_ROPE)
    q = jnp.concatenate([q[..., :MLA_NOPE], rope(q[..., MLA_NOPE:], pos)], axis=-1)
    kv = (rms_norm(ckv, g_kv_lora) @ w_ukv).reshape(B, S, H_MLA, MLA_NOPE + MLA_V)
    k_pe = jnp.broadcast_to(rope(kpe[:, :, None, :], pos), (B, S, H_MLA, MLA_ROPE))
    k = jnp.concatenate([kv[..., :MLA_NOPE], k_pe], axis=-1)
    v = kv[..., MLA_NOPE:]
    o_mla = sweep_attention(q, k, v, (MLA_NOPE + MLA_ROPE) ** -0.5, False).reshape(B, S, -1)

    dq = rope(dq.reshape(B, S, H_DIFF * 2, DIFF_DIM), pos).reshape(B, S, H_DIFF, 2, DIFF_DIM)
    dk = rope(dk.reshape(B, S, H_DIFF * 2, DIFF_DIM), pos).reshape(B, S, H_DIFF, 2, DIFF_DIM)
    dv = dv.reshape(B, S, H_DIFF, 2 * DIFF_DIM)
    a1 = sweep_attention(dq[:, :, :, 0], dk[:, :, :, 0], dv, DIFF_DIM ** -0.5, False)
    a2 = sweep_attention(dq[:, :, :, 1], dk[:, :, :, 1], dv, DIFF_DIM ** -0.5, False)
    lam_init = 0.8 - 0.6 * math.exp(-0.3 * layer_idx)
    lf = diff_lambda.astype(jnp.float32)
    lam = jnp.exp(jnp.sum(lf[0] * lf[1])) - jnp.exp(jnp.sum(lf[2] * lf[3])) + lam_init
    o_diff = rms_norm(a1 - lam.astype(a1.dtype) * a2, g_subln) * (1.0 - lam_init)
    o_diff = o_diff.reshape(B, S, -1)

    o = jnp.concatenate([o_mla, o_diff], axis=-1)
    return o @ w_out


def peer_ffn(x, w_query, sub_keys, u_tab, v_tab):
    B, S, D = x.shape
    T = B * S
    xt = x.reshape(T, D)
    q = (xt @ w_query).reshape(T, PEER_HEADS, 2, PEER_DK // 2)
    s = jnp.einsum('thpd,hpnd->thpn', q, sub_keys, preferred_element_type=jnp.float32)
    sv, si = lax.top_k(s, PEER_TOPK)
    cand = sv[:, :, 0, :, None] + sv[:, :, 1, None, :]
    cidx = si[:, :, 0, :, None] * N_KEYS + si[:, :, 1, None, :]
    top_s, top_pos = lax.top_k(cand.reshape(T, PEER_HEADS, PEER_TOPK * PEER_TOPK), PEER_TOPK)
    idx = jnp.take_along_axis(cidx.reshape(T, PEER_HEADS, PEER_TOPK * PEER_TOPK), top_pos, axis=-1)
    g = jax.nn.softmax(top_s, axis=-1)
    E = PEER_HEADS * PEER_TOPK
    nb = T // PEER_TOKEN_BLOCK

    def block(args):
        xb, ib, gb = args
        h = jax.nn.gelu(jnp.einsum('td,ted->te', xb, u_tab[ib], preferred_element_type=jnp.float32))
        w = (gb * h).astype(v_tab.dtype)
        return jnp.einsum('te,ted->td', w, v_tab[ib])

    out = lax.map(block, (xt.reshape(nb, PEER_TOKEN_BLOCK, D),
                          idx.reshape(nb, PEER_TOKEN_BLOCK, E),
                          g.reshape(nb, PEER_TOKEN_BLOCK, E)))
    return out.reshape(B, S, D)


def setup_inputs(seed: int = 0) -> dict:
    key = jax.random.key(seed)
    ks = jax.random.split(key, 21)
    d = D_MODEL

    def nrm(k, shape, scale):
        return jax.random.normal(k, shape, jnp.float32) * scale

    return {
        'x': nrm(ks[0], (BATCH, SEQ, d), 1.0),
        'w_in_even': nrm(ks[1], (N_EVEN, d, IN_EVEN), d ** -0.5),
        'b_forget': jax.random.uniform(ks[2], (N_EVEN, H_FOX), jnp.float32, 1.0, 4.0),
        'rel_bias': nrm(ks[3], (N_EVEN, H_CHK, 2 * REL_MAX + 1), 0.1),
        'w_out_even': nrm(ks[4], (N_EVEN, MIX_WIDTH_EVEN, d), BETA * MIX_WIDTH_EVEN ** -0.5),
        'w_in_odd': nrm(ks[5], (N_ODD, d, IN_ODD), d ** -0.5),
        'g_q_lora': 1.0 + nrm(ks[6], (N_ODD, Q_LORA), 0.02),
        'g_kv_lora': 1.0 + nrm(ks[7], (N_ODD, KV_LORA), 0.02),
        'w_uq': nrm(ks[8], (N_ODD, Q_LORA, H_MLA * (MLA_NOPE + MLA_ROPE)), Q_LORA ** -0.5),
        'w_ukv': nrm(ks[9], (N_ODD, KV_LORA, H_MLA * (MLA_NOPE + MLA_V)), KV_LORA ** -0.5),
        'diff_lambda': nrm(ks[10], (N_ODD, 4, DIFF_DIM), 0.1),
        'g_subln': 1.0 + nrm(ks[11], (N_ODD, 2 * DIFF_DIM), 0.02),
        'w_out_odd': nrm(ks[12], (N_ODD, MIX_WIDTH_ODD, d), BETA * MIX_WIDTH_ODD ** -0.5),
        'peer_w_query': nrm(ks[13], (DEPTH, d, PEER_HEADS * PEER_DK), d ** -0.5),
        'peer_sub_keys': nrm(ks[14], (DEPTH, PEER_HEADS, 2, N_KEYS, PEER_DK // 2), (PEER_DK // 2) ** -0.5),
        'peer_u': nrm(ks[15], (DEPTH, N_EXPERTS, d), d ** -0.5),
        'peer_v': nrm(ks[16], (DEPTH, N_EXPERTS, d), BETA * (PEER_HEADS * PEER_TOPK) ** -0.5),
        'ln_mix_g': 1.0 + nrm(ks[17], (DEPTH, d), 0.02),
        'ln_mix_b': nrm(ks[18], (DEPTH, d), 0.02),
        'ln_ffn_g': 1.0 + nrm(ks[19], (DEPTH, d), 0.02),
        'ln_ffn_b': nrm(ks[20], (DEPTH, d), 0.02),
    }


def reference(x, w_in_even, b_forget, rel_bias, w_out_even, w_in_odd, g_q_lora, g_kv_lora,
              w_uq, w_ukv, diff_lambda, g_subln, w_out_odd, peer_w_query, peer_sub_keys,
              peer_u, peer_v, ln_mix_g, ln_mix_b, ln_ffn_g, ln_ffn_b):
    pos = jnp.arange(x.shape[1])
    for l in range(DEPTH):
        i = l // 2
        if l % 2 == 0:
            y = even_mixer(x, w_in_even[i], b_forget[i], rel_bias[i], w_out_even[i])
        else:
            y = odd_mixer(x, w_in_odd[i], g_q_lora[i], g_kv_lora[i], w_uq[i], w_ukv[i],
                          diff_lambda[i], g_subln[i], w_out_odd[i], pos, l)
        x = layer_norm(ALPHA * x + y, ln_mix_g[l], ln_mix_b[l])
        y = peer_ffn(x, peer_w_query[l], peer_sub_keys[l], peer_u[l], peer_v[l])
        x = layer_norm(ALPHA * x + y, ln_ffn_g[l], ln_ffn_b[l])
    return x
```

```python
from collections import defaultdict
import math
import os
import numpy as np
import ml_dtypes
import concourse.bass as bass
import concourse.mybir as mybir
from concourse.bass_utils import run_bass_kernel_spmd

F32 = mybir.dt.float32
BF16 = mybir.dt.bfloat16
U32 = mybir.dt.uint32
I32 = mybir.dt.int32
ALU = mybir.AluOpType
AF = mybir.ActivationFunctionType
AX = mybir.AxisListType
NPBF = ml_dtypes.bfloat16

D = 2048
TOK = 1024
NT = 8
DEPTH = 4
ALPHA = (2 * DEPTH) ** 0.25
NEG = -30000.0
NDMA = 8


class KB:
    def __init__(self, nc):
        self.nc = nc
        self.eng = {'pe': nc.tensor, 'act': nc.scalar, 'dve': nc.vector, 'pool': nc.gpsimd, 'sp': nc.sync}
        self.cms = []
        self.sem = {}
        for e in ['pe', 'act', 'dve', 'pool']:
            self.sem[e] = self._mk('s_' + e)
        for q in ['sp', 'pool']:
            for i in range(NDMA):
                self.sem[('d', q, i)] = self._mk('d_%s_%d' % (q, i))
        self.cnt = defaultdict(int)
        self.seen = {e: defaultdict(int) for e in self.eng}
        self.dn = {'sp': 0, 'pool': 0}
        self.lastw = {}
        self.readers = {}
        self.ninst = 0

    def _mk(self, name):
        cm = self.nc.semaphore(name)
        h = cm.__enter__()
        self.cms.append(cm)
        return h

    def close(self):
        for cm in reversed(self.cms):
            cm.__exit__(None, None, None)

    def op(self, e, fn, reads=(), writes=(), dma=False):
        deps = {}

        def add(p):
            if p is not None and deps.get(p[0], 0) < p[1]:
                deps[p[0]] = p[1]
        for r in reads:
            add(self.lastw.get(r))
        for w in writes:
            add(self.lastw.get(w))
            for k, n in self.readers.get(w, {}).items():
                add((k, n))
        if dma:
            sk = ('d', e, self.dn[e] % NDMA)
            self.dn[e] += 1
            if self.cnt[sk] > 0:
                add((sk, self.cnt[sk]))
            inc = 16
        else:
            sk = e
            inc = 1
        eng = self.eng[e]
        for k, n in deps.items():
            if k == 'pe' and e == 'pe':
                continue
            if self.seen[e][k] < n:
                eng.wait_ge(self.sem[k], n)
                self.seen[e][k] = n
                self.ninst += 1
        ins = fn(eng)
        ins.then_inc(self.sem[sk], inc)
        self.ninst += 1
        self.cnt[sk] += inc
        tok = (sk, self.cnt[sk])
        for r in reads:
            self.readers.setdefault(r, {})[sk] = tok[1]
        for w in writes:
            self.lastw[w] = tok
            self.readers[w] = {}
        return tok

    def barrier(self):
        for e, eng in self.eng.items():
            for k, n in list(self.cnt.items()):
                if k == e:
                    continue
                if n > self.seen[e][k]:
                    eng.wait_ge(self.sem[k], n)
                    self.seen[e][k] = n
                    self.ninst += 1

    def finish(self):
        eng = self.eng['sp']
        for k, n in self.cnt.items():
            if n > self.seen['sp'][k]:
                eng.wait_ge(self.sem[k], n)
                self.seen['sp'][k] = n


class Ctx:
    pass


def chunk_pos(j):
    return (j, 0) if j < 4 else (7 - j, 1)


def host_consts():
    c = {}
    c['ident'] = np.eye(128, dtype=np.float32).astype(NPBF)
    c['ones'] = np.ones((128, 128), np.float32).astype(NPBF)
    s = np.arange(128)[:, None]
    t = np.arange(128)[None, :]
    m = np.zeros((3, 128, 128), np.float32)
    m[0][s > t] = NEG
    m[1][(s >= 64) & (t < 64)] = NEG
    m[2][(s < 64) & (t >= 64)] = NEG
    c['masks'] = m.astype(NPBF)
    c['tri'] = (s <= t).astype(np.float32)
    c['onesf'] = np.ones((128, 128), np.float32)
    c['identf'] = np.eye(128, dtype=np.float32)
    r128 = np.zeros((128, 128), np.float32)
    for i in range(64):
        r128[i + 64, i] = -1.0
        r128[i, i + 64] = 1.0
    r64 = np.zeros((128, 128), np.float32)
    for i in range(32):
        r64[i + 32, i] = -1.0
        r64[i, i + 32] = 1.0
    c['rot'] = np.stack([r128, r64]).astype(NPBF)
    c['iota'] = np.tile(np.arange(128, dtype=np.float32)[None, :], (128, 1))
    return c


def rope_tables(tok_pos):
    out = []
    for d in (128, 64):
        inv = 10000.0 ** (-np.arange(0, d, 2, dtype=np.float32) / d)
        ang = tok_pos.astype(np.float32)[None, :] * inv[:, None]
        cos = np.cos(ang).astype(np.float32)
        sin = np.sin(ang).astype(np.float32)
        c2 = np.zeros((128, tok_pos.shape[0]), np.float32)
        s2 = np.zeros((128, tok_pos.shape[0]), np.float32)
        c2[:d // 2] = cos
        c2[d // 2:d] = cos
        s2[:d // 2] = sin
        s2[d // 2:d] = sin
        out += [c2, s2]
    return np.stack(out)


class Prog:
    def __init__(self):
        self.nc = bass.Bass("TRN2", target_bir_lowering=False)
        self.kb = KB(self.nc)
        self.cms = []
        self.uid = 0
        self.ps = self.psum("ps", [128, 8, 512], F32)
        self.ins = {}
        self.outs = {}

    def dram_in(self, name, shape, dt):
        ap = self.nc.dram_tensor(name, list(shape), dt, kind="ExternalInput").ap()
        self.ins[name] = ap
        return ap

    def dram_out(self, name, shape, dt):
        ap = self.nc.dram_tensor(name, list(shape), dt, kind="ExternalOutput").ap()
        self.outs[name] = ap
        return ap

    def dram_tmp(self, name, shape, dt):
        return self.nc.dram_tensor(name, list(shape), dt, kind="Internal").ap()

    def sbuf(self, name, shape, dt):
        self.uid += 1
        cm = self.nc.sbuf_tensor("%s_u%d" % (name, self.uid), list(shape), dt)
        t = cm.__enter__()
        self.cms.append(cm)
        return t

    def psum(self, name, shape, dt):
        cm = self.nc.psum_tensor(name, list(shape), dt)
        t = cm.__enter__()
        self.cms.append(cm)
        return t

    def mark(self):
        return len(self.cms)

    def release(self, mark):
        if len(self.cms) > mark and mark > 0:
            self.kb.barrier()
        while len(self.cms) > mark:
            self.cms.pop().__exit__(None, None, None)

    def end(self):
        self.kb.finish()
        self.release(0)
        self.kb.close()

    def load_consts(self, names):
        hc = host_consts()
        self.hc = {}
        C = {}
        for n in names:
            a = hc[n]
            dt = BF16 if a.dtype == NPBF else F32
            src = self.dram_in("c_" + n, a.shape, dt)
            self.hc["c_" + n] = a
            if a.ndim == 2:
                t = self.sbuf("C_" + n, [128, a.shape[1]], dt)
                self.kb.op('sp', lambda e: e.dma_start(out=t[:], in_=src), writes=['C_' + n], dma=True)
            else:
                t = self.sbuf("C_" + n, [128, a.shape[0], a.shape[2]], dt)
                self.kb.op('sp', lambda e: e.dma_start(out=t[:], in_=src.rearrange("k p c -> p k c")),
                           writes=['C_' + n], dma=True)
            C[n] = t
        self.C = C
        return C


def transpose_to_xT(P, xb, xbkey, xT, xTkey, tcol0, psb):
    kb, ps, C = P.kb, P.ps, P.C
    for g in range(4):
        b = psb[g % len(psb)]
        for k in range(4):
            dc = g * 4 + k
            kb.op('pe', lambda e: e.matmul(ps[:, b, k * 128:(k + 1) * 128], lhsT=xb[:, dc * 128:(dc + 1) * 128],
                                           rhs=C['ident'][:], start=True, stop=True),
                  reads=[xbkey, 'C_ident'], writes=[('ps', b)])
        eng = 'act' if g % 2 else 'dve'
        src = ps[:, b, :].rearrange("p (k c) -> p k c", k=4)
        dst = xT[:, g * 4:(g + 1) * 4, tcol0:tcol0 + 128]
        if eng == 'act':
            kb.op('act', lambda e: e.activation(out=dst, in_=src, func=AF.Copy), reads=[('ps', b)], writes=[xTkey])
        else:
            kb.op('dve', lambda e: e.tensor_copy(out=dst, in_=src), reads=[('ps', b)], writes=[xTkey])


def build_xT(P, xsrc, xT, xTkey, xf_tiles=None):
    kb = P.kb
    m = P.mark()
    xfs = [P.sbuf("bx_xf%d" % i, [128, D], F32) for i in range(2)] if xf_tiles is None else None
    xbs = [P.sbuf("bx_xb%d" % i, [128, D], BF16) for i in range(2)]
    for i in range(NT):
        if xf_tiles is None:
            xf = xfs[i % 2]
            xfk = ('bx_xf', i % 2)
            kb.op('sp', lambda e: e.dma_start(out=xf[:], in_=xsrc[i * 128:(i + 1) * 128, :]), writes=[xfk], dma=True)
        else:
            xf, xfk = xf_tiles[i]
        xb = xbs[i % 2]
        xbk = ('bx_xb', i % 2)
        kb.op('pool', lambda e: e.tensor_copy(out=xb[:], in_=xf[:]), reads=[xfk], writes=[xbk])
        transpose_to_xT(P, xb, xbk, xT, xTkey, i * 128, [0, 1])
    P.release(m)


def load_wblk(P, wblk, key, wsrc, c0, n, kdc):
    P.kb.op('pool', lambda e: e.dma_start(out=wblk[:, 0:kdc, 0:n],
                                          in_=wsrc[:, c0:c0 + n].rearrange("(dc p) c -> p dc c", p=128)),
            writes=[key], dma=True)


def proj_fm(P, xT, xTkey, kdc, wsrc, col_list, sinks, wb, stg):
    kb, ps = P.kb, P.ps
    i = 0
    g = 0
    while i < len(col_list):
        grp = col_list[i:i + 4]
        wblk = wb[g % 2]
        wkey = ('wb', g % 2)
        offs = []
        o = 0
        for (c0, n) in grp:
            offs.append(o)
            o += n
        contiguous = all(grp[k][0] + grp[k][1] == grp[k + 1][0] for k in range(len(grp) - 1))
        if contiguous:
            load_wblk(P, wblk, wkey, wsrc, grp[0][0], o, kdc)
        else:
            for k, (c0, n) in enumerate(grp):
                P.kb.op('pool', lambda e: e.dma_start(out=wblk[:, 0:kdc, offs[k]:offs[k] + n],
                                                      in_=wsrc[:, c0:c0 + n].rearrange("(dc p) c -> p dc c", p=128)),
                        writes=[wkey], dma=True)
        for k, (c0, n) in enumerate(grp):
            for tb in range(2):
                b = 2 + (2 * (i + k) + tb) % 4
                for dc in range(kdc):
                    kb.op('pe', lambda e: e.matmul(ps[0:n, b, :], lhsT=wblk[:, dc, offs[k]:offs[k] + n],
                                                   rhs=xT[:, dc, tb * 512:(tb + 1) * 512],
                                                   start=(dc == 0), stop=(dc == kdc - 1)),
                          reads=[wkey, xTkey], writes=[('ps', b)])
                sinks[i + k](ps[0:n, b, :], tb, ('ps', b))
        i += 4
        g += 1


def proj_tm(P, xT, xTkey, kdc, wsrc, c0, ncols, sink, wb, gsel=None):
    kb, ps = P.kb, P.ps
    nb = (ncols + 511) // 512
    for cb in range(nb):
        n = min(512, ncols - cb * 512)
        wblk = wb[cb % 2]
        wkey = ('wb', cb % 2)
        if gsel is None:
            load_wblk(P, wblk, wkey, wsrc, c0 + cb * 512, n, kdc)
        else:
            gsel(wblk, wkey, cb)
        for i in range(NT):
            b = 2 + (cb * NT + i) % 4
            for dc in range(kdc):
                kb.op('pe', lambda e: e.matmul(ps[:, b, 0:n], lhsT=xT[:, dc, i * 128:(i + 1) * 128],
                                               rhs=wblk[:, dc, 0:n], start=(dc == 0), stop=(dc == kdc - 1)),
                      reads=[wkey, xTkey], writes=[('ps', b)])
            sink(ps[:, b, 0:n], i, cb, n, ('ps', b))


class Stager:
    def __init__(self, P, name, n=3, width=512, dt=BF16):
        self.P = P
        self.name = name
        self.t = [P.sbuf("%s%d" % (name, i), [128, width], dt) for i in range(n)]
        self.i = 0

    def next(self):
        k = self.i % len(self.t)
        self.i += 1
        return self.t[k], (self.name, k)


def evac(P, eng, dst, src, reads, writes, scale=None):
    kb = P.kb
    if eng == 'act':
        if scale is None:
            kb.op('act', lambda e: e.activation(out=dst, in_=src, func=AF.Copy), reads=reads, writes=writes)
        else:
            kb.op('act', lambda e: e.activation(out=dst, in_=src, func=AF.Copy, scale=scale), reads=reads, writes=writes)
    else:
        if scale is None:
            kb.op('dve', lambda e: e.tensor_copy(out=dst, in_=src), reads=reads, writes=writes)
        else:
            kb.op('dve', lambda e: e.tensor_scalar(out=dst, in0=src, scalar1=scale, scalar2=None, op0=ALU.mult),
                  reads=reads, writes=writes)


def stage_a_even(P, xsrc, w_in, bfg, QT, KT, V, LF):
    kb = P.kb
    m = P.mark()
    xT = P.sbuf("xT", [128, 16, TOK], BF16)
    build_xT(P, xsrc, xT, 'xT')
    wb = [P.sbuf("wb%d" % i, [128, 16, 512], BF16) for i in range(2)]
    st = Stager(P, "stg", 4)
    sc = 128.0 ** -0.5
    cnt = [0]

    def mk_sink(dst, ch, scale):
        def sink(psap, tb, pskey):
            t, tk = st.next()
            eng = 'act' if cnt[0] % 2 else 'dve'
            cnt[0] += 1
            evac(P, eng, t[:, :], psap, [pskey], [tk], scale)
            kb.op('sp', lambda e: e.dma_start(out=dst[ch, :, tb * 512:(tb + 1) * 512], in_=t[:, :]),
                  reads=[tk], writes=[], dma=True)
        return sink
    cols = []
    sinks = []
    for ch in range(8):
        cols.append((ch * 128, 128)); sinks.append(mk_sink(QT, ch, sc))
    for ch in range(8):
        cols.append((1024 + ch * 128, 128)); sinks.append(mk_sink(KT, ch, None))
    for ch in range(8):
        cols.append((3080 + ch * 128, 128)); sinks.append(mk_sink(QT, 8 + ch, sc))
    for ch in range(8):
        cols.append((4104 + ch * 128, 128)); sinks.append(mk_sink(KT, 8 + ch, None))
    proj_fm(P, xT, 'xT', 16, w_in, cols, sinks, wb, st)

    def mk_vsink(voff):
        def vsink(psap, i, cb, n, pskey):
            t, tk = st.next()
            eng = 'act' if cnt[0] % 2 else 'dve'
            cnt[0] += 1
            evac(P, eng, t[:, 0:n], psap, [pskey], [tk])
            kb.op('sp', lambda e: e.dma_start(out=V[i * 128:(i + 1) * 128, voff + cb * 512:voff + cb * 512 + n],
                                              in_=t[:, 0:n]), reads=[tk], writes=[], dma=True)
        return vsink
    proj_tm(P, xT, 'xT', 16, w_in, 2048, 1024, mk_vsink(0), wb)
    proj_tm(P, xT, 'xT', 16, w_in, 5128, 1024, mk_vsink(1024), wb)
    bt = P.sbuf("bfg_t", [128, 8], F32)
    kb.op('sp', lambda e: e.dma_start(out=bt[:], in_=bfg.partition_broadcast(128)), writes=['bfg_t'], dma=True)
    lft = [P.sbuf("lf%d" % i, [128, 8], F32) for i in range(2)]

    def fsink(psap, i, cb, n, pskey):
        t = lft[i % 2]
        tk = ('lf', i % 2)
        kb.op('dve', lambda e: e.tensor_tensor(out=t[:], in0=psap, in1=bt[:], op=ALU.add),
              reads=[pskey, 'bfg_t'], writes=[tk])
        kb.op('act', lambda e: e.activation(out=t[:], in_=t[:], func=AF.Exp, scale=-1.0), reads=[tk], writes=[tk])
        kb.op('act', lambda e: e.activation(out=t[:], in_=t[:], func=AF.Ln, bias=1.0), reads=[tk], writes=[tk])
        kb.op('act', lambda e: e.mul(out=t[:], in_=t[:], mul=-1.0), reads=[tk], writes=[tk])
        kb.op('sp', lambda e: e.dma_start(out=LF[i * 128:(i + 1) * 128, :], in_=t[:]), reads=[tk], dma=True)
    proj_tm(P, xT, 'xT', 16, w_in, 3072, 8, fsink, wb)
    P.release(m)


def build_a_even():
    P = Prog()
    P.load_consts(['ident'])
    xsrc = P.dram_in("x", [TOK, D], F32)
    w_in = P.dram_in("w_in", [D, 6152], F32)
    bfg = P.dram_in("bfg", [8], F32)
    QT = P.dram_out("QT", [16, 128, TOK], BF16)
    KT = P.dram_out("KT", [16, 128, TOK], BF16)
    V = P.dram_out("V", [TOK, 2048], BF16)
    LF = P.dram_out("LF", [TOK, 8], F32)
    stage_a_even(P, xsrc, w_in, bfg, QT, KT, V, LF)
    P.end()
    return P


def fox_bias_tables(P, LFown, LFoth, visb):
    kb, ps, C = P.kb, P.ps, P.C
    nb_own = P.sbuf("nb_own", [128, 8, 8, 8], F32)
    nb_oth = P.sbuf("nb_oth", [128, 8, 8, 24], F32)
    m = P.mark()
    lfa = P.sbuf("lfa", [128, 32, 8], F32)
    hi = P.sbuf("lf_hi", [128, 32, 8], BF16)
    lo = P.sbuf("lf_lo", [128, 32, 8], BF16)
    rem = P.sbuf("lf_rem", [128, 32, 8], F32)
    lcum = P.sbuf("lcum", [128, 8, 8], F32)
    chi = P.sbuf("lc_hi", [128, 64], BF16)
    clo = P.sbuf("lc_lo", [128, 64], BF16)
    crem = P.sbuf("lc_rem", [128, 64], F32)
    dd = P.sbuf("dd", [128, 24, 8], F32)
    tot = P.sbuf("tot", [128, 8], F32)
    lct = P.sbuf("lct", [128, 8, 8], F32)
    kb.op('sp', lambda e: e.dma_start(out=lfa[:, 0:8, :], in_=LFown.rearrange("(k p) h -> p k h", p=128)), writes=['lfa'], dma=True)
    kb.op('sp', lambda e: e.dma_start(out=lfa[:, 8:32, :], in_=LFoth.rearrange("(k p) h -> p k h", p=128)), writes=['lfa'], dma=True)
    kb.op('dve', lambda e: e.tensor_copy(out=hi[:], in_=lfa[:]), reads=['lfa'], writes=['lfhi'])
    kb.op('dve', lambda e: e.tensor_tensor(out=rem[:], in0=lfa[:], in1=hi[:], op=ALU.subtract), reads=['lfa', 'lfhi'], writes=['lfrem'])
    kb.op('dve', lambda e: e.tensor_copy(out=lo[:], in_=rem[:]), reads=['lfrem'], writes=['lflo'])
    kb.op('dve', lambda e: e.memset(tot[:], 0.0), writes=['tot'])

    def pair(dst, mat, key, i):
        kb.op('pe', lambda e: e.matmul(dst, lhsT=C[mat][:], rhs=hi[:, i, :], start=True, stop=False),
              reads=['lfhi', 'C_' + mat], writes=[key])
        kb.op('pe', lambda e: e.matmul(dst, lhsT=C[mat][:], rhs=lo[:, i, :], start=False, stop=True),
              reads=['lflo', 'C_' + mat], writes=[key])
    for i in range(32):
        if i == 8:
            kb.op('dve', lambda e: e.memset(tot[:], 0.0), reads=[], writes=['tot'])
        pair(ps[:, 6, 0:8], 'trib' if i < 8 else 'supb', ('ps', 6), i)
        pair(ps[:, 6, 8:16], 'ones', ('ps', 6), i)
        dst = lcum[:, i, :] if i < 8 else dd[:, i - 8, :]
        dk = 'lcum' if i < 8 else 'dd'
        kb.op('dve', lambda e: e.tensor_tensor(out=dst, in0=ps[:, 6, 0:8], in1=tot[:], op=ALU.add),
              reads=[('ps', 6), 'tot'], writes=[dk])
        kb.op('dve', lambda e: e.tensor_tensor(out=tot[:], in0=ps[:, 6, 8:16], in1=tot[:], op=ALU.add),
              reads=[('ps', 6), 'tot'], writes=['tot'])
    for h in range(8):
        kb.op('dve', lambda e: e.tensor_tensor(out=dd[:, :, h], in0=dd[:, :, h], in1=visb[:], op=ALU.add),
              reads=['dd', 'visb'], writes=['dd'])
    lc2 = lcum[:].rearrange("p k h -> p (k h)")
    kb.op('dve', lambda e: e.tensor_copy(out=chi[:], in_=lc2), reads=['lcum'], writes=['chi'])
    kb.op('dve', lambda e: e.tensor_tensor(out=crem[:], in0=lc2, in1=chi[:], op=ALU.subtract), reads=['lcum', 'chi'], writes=['crem'])
    kb.op('dve', lambda e: e.tensor_copy(out=clo[:], in_=crem[:]), reads=['crem'], writes=['clo'])
    kb.op('pe', lambda e: e.matmul(ps[:, 6, 0:64], lhsT=C['sel63b'][:], rhs=chi[:], start=True, stop=False),
          reads=['chi', 'C_sel63b'], writes=[('ps', 6)])
    kb.op('pe', lambda e: e.matmul(ps[:, 6, 0:64], lhsT=C['sel63b'][:], rhs=clo[:], start=False, stop=True),
          reads=['clo', 'C_sel63b'], writes=[('ps', 6)])
    kb.op('dve', lambda e: e.tensor_copy(out=lct[:].rearrange("p k h -> p (k h)"), in_=ps[:, 6, 0:64]),
          reads=[('ps', 6)], writes=['lct'])
    for h in range(8):
        for qt in range(8):
            kb.op('dve', lambda e: e.tensor_scalar(out=nb_own[:, h, qt, :], in0=lcum[:, :, h], scalar1=lct[:, qt, h:h + 1],
                                                   scalar2=-1.0, op0=ALU.subtract, op1=ALU.mult),
                  reads=['lcum', 'lct'], writes=['nb_own'])
            kb.op('dve', lambda e: e.tensor_scalar(out=nb_oth[:, h, qt, :], in0=dd[:, :, h], scalar1=lct[:, qt, h:h + 1],
                                                   scalar2=None, op0=ALU.add),
                  reads=['dd', 'lct'], writes=['nb_oth'])
    P.release(m)
    return nb_own, nb_oth


def attention(P, units, QT, Kown, Koth, Vown, Voth, visb, sink, nb=None, bmat=None, sink_gets=False):
    kb, ps, C = P.kb, P.ps, P.C
    m = P.mark()
    NQ = max(len(u['q']) for u in units)
    DV = max(u['dv'] for u in units)
    bufs = []
    for i in range(2):
        bufs.append(dict(
            q=P.sbuf("at_q%d" % i, [128, NQ, TOK], BF16),
            ko=P.sbuf("at_ko%d" % i, [128, NQ, TOK], BF16),
            kt=P.sbuf("at_kt%d" % i, [128, NQ, 3072], BF16),
            vo=P.sbuf("at_vo%d" % i, [128, 8, DV], BF16),
            vt=P.sbuf("at_vt%d" % i, [128, 24, DV], BF16)))
    pts = [P.sbuf("at_p%d" % i, [128, 128], BF16) for i in range(4)]
    rinvs = [P.sbuf("at_r%d" % i, [128, 128], F32) for i in range(2)]
    npair = 0
    nps = [0]

    def get_s():
        slot = nps[0] % 2
        nps[0] += 1
        return ps[:, slot, 0:128], ('S', slot)
    for ui, u in enumerate(units):
        B = bufs[ui % 2]
        bk = ('atb', ui % 2)
        nq = len(u['q'])
        dv = u['dv']
        nch = dv // 128
        for i, qc in enumerate(u['q']):
            kb.op('sp', lambda e: e.dma_start(out=B['q'][:, i, :], in_=QT[qc]), writes=[bk], dma=True)
        for i, kc in enumerate(u['k']):
            kb.op('sp', lambda e: e.dma_start(out=B['ko'][:, i, :], in_=Kown[kc]), writes=[bk], dma=True)
            if u['mode'] == 'band':
                kb.op('sp', lambda e: e.dma_start(out=B['kt'][:, i, 0:512], in_=Koth[kc][:, 0:512]), writes=[bk], dma=True)
            else:
                kb.op('sp', lambda e: e.dma_start(out=B['kt'][:, i, :], in_=Koth[kc]), writes=[bk], dma=True)
        kb.op('sp', lambda e: e.dma_start(out=B['vo'][:, :, 0:dv],
                                          in_=Vown[:, u['v0']:u['v0'] + dv].rearrange("(k p) c -> p k c", p=128)),
              writes=[bk], dma=True)
        nvt = 4 if u['mode'] == 'band' else 24
        kb.op('sp', lambda e: e.dma_start(out=B['vt'][:, 0:nvt, 0:dv],
                                          in_=Voth[0:nvt * 128, u['v0']:u['v0'] + dv].rearrange("(k p) c -> p k c", p=128)),
              writes=[bk], dma=True)
        for qt in range(8):
            if u['mode'] == 'band':
                keys = []
                for uu in range(qt - 4, qt + 1):
                    d = qt - uu
                    keys.append((uu, d))
            else:
                keys = [(-1 - k, None) for k in range(23, -1, -1)] + [(uu, None) for uu in range(qt + 1)]
            par = (ui * 8 + qt) % 2
            Oacc = [ps[:, 2 + 3 * par + c, 0:128] for c in range(nch)]
            Lacc = ps[:, 4 + 3 * par, 0:128]
            acck = ('acc', par)
            for ki, (uu, d) in enumerate(keys):
                first = (ki == 0)
                last = (ki == len(keys) - 1)
                npair += 1
                S, sk = get_s()
                own = uu >= 0
                kidx = uu if own else (-1 - uu)
                ksrc = B['ko'] if own else B['kt']
                vsrc = B['vo'] if own else B['vt']
                extra = []
                if u['mode'] == 'causal' and own and uu == qt:
                    extra.append(C['masks'][:, 0, :])
                if u['mode'] == 'chunk' and own and uu == qt:
                    extra.append(C['masks'][:, 1, :])
                if u['mode'] == 'band':
                    if d == 4:
                        extra += [C['masks'][:, 2, :], bmat[:, u['h'], 2, :]]
                    elif d in (2, 3):
                        extra.append(bmat[:, u['h'], 2, :])
                    elif d == 1:
                        extra.append(bmat[:, u['h'], 1, :])
                    else:
                        extra += [C['masks'][:, 1, :], bmat[:, u['h'], 0, :]]
                nmm = nq + len(extra)
                for i in range(nq):
                    kb.op('pe', lambda e: e.matmul(S, lhsT=ksrc[:, i, kidx * 128:(kidx + 1) * 128],
                                                   rhs=B['q'][:, i, qt * 128:(qt + 1) * 128],
                                                   start=(i == 0), stop=(i == nmm - 1)),
                          reads=[bk], writes=[sk])
                for j, xm in enumerate(extra):
                    kb.op('pe', lambda e: e.matmul(S, lhsT=C['ident'][:], rhs=xm, start=False, stop=(nq + j == nmm - 1)),
                          reads=['C_ident', 'C_masks', 'bmat'], writes=[sk])
                pt = pts[npair % 4]
                pk = ('pt', npair % 4)
                if nb is not None and u['mode'] == 'causal':
                    bias = (nb[0][:, u['h'], qt, kidx:kidx + 1] if own else nb[1][:, u['h'], qt, kidx:kidx + 1])
                    rk = ['nb_own', 'nb_oth']
                elif not own:
                    bias = visb[:, kidx:kidx + 1]
                    rk = ['visb']
                else:
                    bias = 0.0
                    rk = []
                kb.op('act', lambda e: e.activation(out=pt[:], in_=S, func=AF.Exp, bias=bias), reads=[sk] + rk, writes=[pk])
                for c in range(nch):
                    kb.op('pe', lambda e: e.matmul(Oacc[c], lhsT=vsrc[:, kidx, c * 128:(c + 1) * 128], rhs=pt[:],
                                                   start=first, stop=last), reads=[bk, pk], writes=[acck])
                kb.op('pe', lambda e: e.matmul(Lacc, lhsT=C['ones'][:], rhs=pt[:], start=first, stop=last),
                      reads=['C_ones', pk], writes=[acck])
            rinv = rinvs[par]
            rk = ('rinv', par)
            kb.op('dve', lambda e: e.reciprocal(out=rinv[:], in_=Lacc), reads=[acck], writes=[rk])
            for c in range(nch):
                if sink_gets:
                    sink(ui, qt, c, Oacc[c], rinv, [acck, rk], get_s)
                else:
                    sink(ui, qt, c, Oacc[c], rinv, [acck, rk])
    P.release(m)


def even_units():
    us = []
    for h in range(8):
        us.append(dict(q=[h], k=[h], v0=h * 128, dv=128, mode='causal', h=h, fc=h))
    for h in range(8):
        us.append(dict(q=[8 + h], k=[8 + h], v0=1024 + h * 128, dv=128, mode='band', h=h, fc=8 + h))
    return us


def stage_b_even(P, QT, Kown, Koth, Vown, Voth, LFown, LFoth, visb_d, bias_d, oT):
    kb, C = P.kb, P.C
    visb = P.sbuf("visb", [128, 24], F32)
    kb.op('sp', lambda e: e.dma_start(out=visb[:], in_=visb_d), writes=['visb'], dma=True)
    bmat = P.sbuf("bmat", [128, 8, 3, 128], BF16)
    kb.op('pool', lambda e: e.dma_start(out=bmat[:], in_=bias_d.rearrange("h k p c -> p h k c")), writes=['bmat'], dma=True)
    nb = fox_bias_tables(P, LFown, LFoth, visb)
    units = even_units()

    def sink(ui, qt, c, O, rinv, keys):
        fc = units[ui]['fc'] + c
        kb.op('dve', lambda e: e.tensor_tensor(out=oT[:, fc, qt * 128:(qt + 1) * 128], in0=O, in1=rinv[:], op=ALU.mult),
              reads=keys, writes=['oT'])
    attention(P, units, QT, Kown, Koth, Vown, Voth, visb, sink, nb=nb, bmat=bmat)


def host_consts2():
    c = {}
    s = np.arange(128)[:, None]
    t = np.arange(128)[None, :]
    c['sup'] = (s > t).astype(np.float32)
    sel = np.zeros((128, 128), np.float32)
    sel[63, :] = 1.0
    c['sel63'] = sel
    c['sel63b'] = sel.astype(NPBF)
    c['iota16'] = np.tile(np.arange(16, dtype=np.float32)[None, :], (128, 128))
    c['trib'] = (s <= t).astype(np.float32).astype(NPBF)
    c['supb'] = (s > t).astype(np.float32).astype(NPBF)
    return c


_hc = host_consts


def host_consts():
    c = _hc()
    c.update(host_consts2())
    return c


def build_b_even_test():
    P = Prog()
    P.load_consts(['ident', 'ones', 'masks', 'trib', 'supb', 'sel63b'])
    QT = P.dram_in("QT", [16, 128, TOK], BF16)
    Kown = P.dram_in("Kown", [16, 128, TOK], BF16)
    Koth = P.dram_in("Koth", [16, 128, 3072], BF16)
    Vown = P.dram_in("Vown", [TOK, 2048], BF16)
    Voth = P.dram_in("Voth", [3072, 2048], BF16)
    LFown = P.dram_in("LFown", [TOK, 8], F32)
    LFoth = P.dram_in("LFoth", [3072, 8], F32)
    visb = P.dram_in("visb", [128, 24], F32)
    biasd = P.dram_in("biasd", [8, 3, 128, 128], F32)
    oTd = P.dram_out("oT", [16, 128, TOK], BF16)
    oT = P.sbuf("oT", [128, 16, TOK], BF16)
    stage_b_even(P, QT, Kown, Koth, Vown, Voth, LFown, LFoth, visb, biasd, oT)
    P.kb.op('sp', lambda e: e.dma_start(out=oTd.rearrange("f p t -> p f t"), in_=oT[:]), reads=['oT'], dma=True)
    P.end()
    return P


def others_order(r):
    M0 = 8 * r
    vis = [M0 - 1 - k for k in range(M0)]
    inv = [m for m in range(M0 + 8, 32)]
    return vis + inv, len(vis)


def band_bias_mats(rel_bias_l):
    s = np.arange(128)[:, None]
    t = np.arange(128)[None, :]
    i0 = np.clip(t - s, -128, 128) + 128
    i1 = np.clip(t - s + 128, -128, 128) + 128
    i2 = np.full((128, 128), 256)
    idx = np.stack([i0, i1, i2])
    return np.ascontiguousarray(rel_bias_l[:, idx])


def layer_norm_tiles(P, xt, key_fn, g_d, b_d, eps, dst_fn=None):
    kb = P.kb
    m = P.mark()
    gb = P.sbuf("ln_g", [128, D], F32)
    bb = P.sbuf("ln_b", [128, D], F32)
    tmp = P.sbuf("ln_tmp", [128, D], F32)
    st = P.sbuf("ln_st", [128, 8], F32)
    kb.op('sp', lambda e: e.dma_start(out=gb[:], in_=g_d.partition_broadcast(128)), writes=['ln_g'], dma=True)
    kb.op('sp', lambda e: e.dma_start(out=bb[:], in_=b_d.partition_broadcast(128)), writes=['ln_b'], dma=True)
    for i in range(NT):
        x = xt[:, i, :]
        k = key_fn(i)
        kb.op('dve', lambda e: e.reduce_sum(out=st[:, 0:1], in_=x, axis=AX.X), reads=[k], writes=['ln_st'])
        kb.op('dve', lambda e: e.tensor_scalar(out=st[:, 1:2], in0=st[:, 0:1], scalar1=-1.0 / D, scalar2=None, op0=ALU.mult),
              reads=['ln_st'], writes=['ln_st'])
        kb.op('dve', lambda e: e.tensor_scalar(out=x, in0=x, scalar1=st[:, 1:2], scalar2=None, op0=ALU.add),
              reads=[k, 'ln_st'], writes=[k])
        kb.op('pool', lambda e: e.tensor_tensor(out=tmp[:], in0=x, in1=x, op=ALU.mult), reads=[k], writes=['ln_tmp'])
        kb.op('dve', lambda e: e.reduce_sum(out=st[:, 2:3], in_=tmp[:], axis=AX.X), reads=['ln_tmp'], writes=['ln_st'])
        kb.op('dve', lambda e: e.tensor_scalar(out=st[:, 3:4], in0=st[:, 2:3], scalar1=1.0 / D, scalar2=eps,
                                               op0=ALU.mult, op1=ALU.add), reads=['ln_st'], writes=['ln_st'])
        kb.op('act', lambda e: e.activation(out=st[:, 4:5], in_=st[:, 3:4], func=AF.Sqrt), reads=['ln_st'], writes=['ln_st'])
        kb.op('dve', lambda e: e.reciprocal(out=st[:, 5:6], in_=st[:, 4:5]), reads=['ln_st'], writes=['ln_st'])
        kb.op('dve', lambda e: e.scalar_tensor_tensor(out=x, in0=x, scalar=st[:, 5:6], in1=gb[:], op0=ALU.mult, op1=ALU.mult),
              reads=[k, 'ln_st', 'ln_g'], writes=[k])
        kb.op('pool', lambda e: e.tensor_tensor(out=x, in0=x, in1=bb[:], op=ALU.add), reads=[k, 'ln_b'], writes=[k])
        if dst_fn is not None:
            dst_fn(i, x, k)
    P.release(m)


def stage_c1(P, oT, xsrc, w_out, g_d, b_d, X1d):
    kb = P.kb
    m = P.mark()
    x1 = P.sbuf("x1", [128, NT, D], F32)
    wb = [P.sbuf("wbc%d" % i, [128, 16, 512], BF16) for i in range(2)]
    for i in range(NT):
        kb.op('sp', lambda e: e.dma_start(out=x1[:, i, :], in_=xsrc[i * 128:(i + 1) * 128, :]), writes=[('x1', i)], dma=True)

    def sink(psap, i, cb, n, pskey):
        dst = x1[:, i, cb * 512:cb * 512 + n]
        kb.op('dve', lambda e: e.scalar_tensor_tensor(out=dst, in0=dst, scalar=ALPHA, in1=psap, op0=ALU.mult, op1=ALU.add),
              reads=[pskey, ('x1', i)], writes=[('x1', i)])
    proj_tm(P, oT, 'oT', 16, w_out, 0, D, sink, wb)

    def store(i, x, k):
        kb.op('sp', lambda e: e.dma_start(out=X1d[i * 128:(i + 1) * 128, :], in_=x), reads=[k], writes=['X1d'], dma=True)
    layer_norm_tiles(P, x1, lambda i: ('x1', i), g_d, b_d, 1e-5, store)
    P.release(m)


def peer_select(P, X1d, w_query, skT_d, Gd):
    kb, ps, C = P.kb, P.ps, P.C
    m0 = P.mark()
    qpT = P.sbuf("qpT", [128, 16, TOK], BF16)
    m1 = P.mark()
    xT = P.sbuf("xTq", [128, 16, TOK], BF16)
    build_xT(P, X1d, xT, 'xTq')
    wb = [P.sbuf("wbq%d" % i, [128, 16, 512], BF16) for i in range(2)]
    cnt = [0]

    def mk_sink(hp):
        def sink(psap, tb, pskey):
            eng = 'act' if cnt[0] % 2 else 'dve'
            cnt[0] += 1
            evac(P, eng, qpT[:, hp, tb * 512:(tb + 1) * 512], psap, [pskey], ['qpT'])
        return sink
    proj_fm(P, xT, 'xTq', 16, w_query, [(hp * 128, 128) for hp in range(16)], [mk_sink(hp) for hp in range(16)], wb, None)
    P.release(m1)
    skT = P.sbuf("skT", [128, 16, 128], BF16)
    kb.op('pool', lambda e: e.dma_start(out=skT[:], in_=skT_d.rearrange("h d n -> d h n")), writes=['skT'], dma=True)
    sc = P.sbuf("pk_sc", [128, 16, 128], F32)
    tmp = P.sbuf("pk_tmp", [128, 256], F32)
    sv = P.sbuf("pk_sv", [128, 16, 16], F32)
    si = P.sbuf("pk_si", [128, 16, 16], U32)
    sif = P.sbuf("pk_sif", [128, 16, 16], F32)
    cand = P.sbuf("pk_cand", [128, 8, 16, 16], F32)
    ts = P.sbuf("pk_ts", [128, 8, 16], F32)
    tp = P.sbuf("pk_tp", [128, 8, 16], U32)
    k0u = P.sbuf("pk_k0u", [128, 8, 16], U32)
    k1u = P.sbuf("pk_k1u", [128, 8, 16], U32)
    k0f = P.sbuf("pk_k0f", [128, 8, 16], F32)
    k1f = P.sbuf("pk_k1f", [128, 8, 16], F32)
    oh = P.sbuf("pk_oh", [128, 8, 16, 16], F32)
    im = P.sbuf("pk_im", [128, 8, 16], F32)
    jm = P.sbuf("pk_jm", [128, 8, 16], F32)
    gt = P.sbuf("pk_g", [128, 8, 16], F32)
    zz = P.sbuf("pk_z", [128, 8], F32)
    tb3 = P.sbuf("pk_b3", [128, 3, 128], BF16)
    mT = P.sbuf("pk_mT", [128, 3, 128], F32)
    Poh = [P.sbuf("pk_P%d" % i, [128, 64, 128], BF16) for i in range(2)]
    Qoh = [P.sbuf("pk_Q%d" % i, [128, 64, 128], BF16) for i in range(2)]
    GT = P.sbuf("pk_GT", [128, 128, 128], BF16)
    svv = sv[:].rearrange("p (h two) k -> p h two k", two=2)
    sfv = sif[:].rearrange("p (h two) k -> p h two k", two=2)
    for it in range(NT):
        tcols = slice(it * 128, (it + 1) * 128)
        for g in range(4):
            b = 2 + g
            for k in range(4):
                hp = g * 4 + k
                kb.op('pe', lambda e: e.matmul(ps[:, b, k * 128:(k + 1) * 128], lhsT=qpT[:, hp, tcols], rhs=skT[:, hp, :],
                                               start=True, stop=True), reads=['qpT', 'skT'], writes=[('ps', b)])
            evac(P, 'act' if g % 2 else 'dve', sc[:, g * 4:(g + 1) * 4, :],
                 ps[:, b, :].rearrange("p (k c) -> p k c", k=4), [('ps', b)], ['pk_sc'])
        for s_ in range(16):
            kb.op('dve', lambda e: e.max(out=sv[:, s_, 0:8], in_=sc[:, s_, :]), reads=['pk_sc'], writes=['pk_sv'])
            kb.op('dve', lambda e: e.max_index(out=si[:, s_, 0:8], in_max=sv[:, s_, 0:8], in_values=sc[:, s_, :]),
                  reads=['pk_sc', 'pk_sv'], writes=['pk_si'])
            kb.op('dve', lambda e: e.match_replace(out=tmp[:, 0:128], in_to_replace=sv[:, s_, 0:8], in_values=sc[:, s_, :],
                                                   imm_value=-1e30), reads=['pk_sc', 'pk_sv'], writes=['pk_tmp'])
            kb.op('dve', lambda e: e.max(out=sv[:, s_, 8:16], in_=tmp[:, 0:128]), reads=['pk_tmp'], writes=['pk_sv'])
            kb.op('dve', lambda e: e.max_index(out=si[:, s_, 8:16], in_max=sv[:, s_, 8:16], in_values=tmp[:, 0:128]),
                  reads=['pk_tmp', 'pk_sv'], writes=['pk_si'])
        kb.op('dve', lambda e: e.tensor_copy(out=sif[:], in_=si[:]), reads=['pk_si'], writes=['pk_sif'])
        kb.op('dve', lambda e: e.tensor_tensor(out=cand[:], in0=svv[:, :, 0, :].unsqueeze(3).broadcast_to([128, 8, 16, 16]),
                                               in1=svv[:, :, 1, :].unsqueeze(2).broadcast_to([128, 8, 16, 16]), op=ALU.add),
              reads=['pk_sv'], writes=['pk_cand'])
        for h in range(8):
            cv = cand[:, h, :, :].rearrange("p a b -> p (a b)")
            kb.op('dve', lambda e: e.max(out=ts[:, h, 0:8], in_=cv), reads=['pk_cand'], writes=['pk_ts'])
            kb.op('dve', lambda e: e.max_index(out=tp[:, h, 0:8], in_max=ts[:, h, 0:8], in_values=cv),
                  reads=['pk_cand', 'pk_ts'], writes=['pk_tp'])
            kb.op('dve', lambda e: e.match_replace(out=tmp[:], in_to_replace=ts[:, h, 0:8], in_values=cv, imm_value=-1e30),
                  reads=['pk_cand', 'pk_ts'], writes=['pk_tmp'])
            kb.op('dve', lambda e: e.max(out=ts[:, h, 8:16], in_=tmp[:]), reads=['pk_tmp'], writes=['pk_ts'])
            kb.op('dve', lambda e: e.max_index(out=tp[:, h, 8:16], in_max=ts[:, h, 8:16], in_values=tmp[:]),
                  reads=['pk_tmp', 'pk_ts'], writes=['pk_tp'])
        kb.op('dve', lambda e: e.tensor_single_scalar(out=k0u[:], in_=tp[:], scalar=4, op=ALU.logical_shift_right),
              reads=['pk_tp'], writes=['pk_k0u'])
        kb.op('dve', lambda e: e.tensor_single_scalar(out=k1u[:], in_=tp[:], scalar=15, op=ALU.bitwise_and),
              reads=['pk_tp'], writes=['pk_k1u'])
        kb.op('dve', lambda e: e.tensor_copy(out=k0f[:], in_=k0u[:]), reads=['pk_k0u'], writes=['pk_k0f'])
        kb.op('dve', lambda e: e.tensor_copy(out=k1f[:], in_=k1u[:]), reads=['pk_k1u'], writes=['pk_k1f'])
        for (kf, kfk, p_, dst, dk) in ((k0f, 'pk_k0f', 0, im, 'pk_im'), (k1f, 'pk_k1f', 1, jm, 'pk_jm')):
            kb.op('dve', lambda e: e.tensor_tensor(out=oh[:], in0=C['iota16'][:].rearrange("p (a b c) -> p a b c", a=8, b=16),
                                                   in1=kf[:].unsqueeze(3).broadcast_to([128, 8, 16, 16]), op=ALU.is_equal),
                  reads=[kfk, 'C_iota16'], writes=['pk_oh'])
            kb.op('dve', lambda e: e.tensor_tensor(out=oh[:], in0=oh[:],
                                                   in1=sfv[:, :, p_, :].unsqueeze(2).broadcast_to([128, 8, 16, 16]), op=ALU.mult),
                  reads=['pk_oh', 'pk_sif'], writes=['pk_oh'])
            kb.op('dve', lambda e: e.reduce_sum(out=dst[:], in_=oh[:], axis=AX.X), reads=['pk_oh'], writes=[dk])
        kb.op('dve', lambda e: e.tensor_tensor(out=gt[:], in0=ts[:], in1=ts[:, :, 0:1].broadcast_to([128, 8, 16]), op=ALU.subtract),
              reads=['pk_ts'], writes=['pk_g'])
        kb.op('act', lambda e: e.activation(out=gt[:], in_=gt[:], func=AF.Exp), reads=['pk_g'], writes=['pk_g'])
        kb.op('dve', lambda e: e.reduce_sum(out=zz[:], in_=gt[:], axis=AX.X), reads=['pk_g'], writes=['pk_z'])
        kb.op('dve', lambda e: e.reciprocal(out=zz[:], in_=zz[:]), reads=['pk_z'], writes=['pk_z'])
        kb.op('dve', lambda e: e.tensor_tensor(out=gt[:], in0=gt[:], in1=zz[:].unsqueeze(2).broadcast_to([128, 8, 16]), op=ALU.mult),
              reads=['pk_g', 'pk_z'], writes=['pk_g'])
        for q, (src, sk_) in enumerate(((im, 'pk_im'), (jm, 'pk_jm'), (gt, 'pk_g'))):
            kb.op('dve', lambda e: e.tensor_copy(out=tb3[:, q, :], in_=src[:].rearrange("p h k -> p (h k)")),
                  reads=[sk_], writes=['pk_b3'])
        for q in range(3):
            kb.op('pe', lambda e: e.matmul(ps[:, 6, q * 128:(q + 1) * 128], lhsT=tb3[:, q, :], rhs=C['ident'][:],
                                           start=True, stop=True), reads=['pk_b3', 'C_ident'], writes=[('ps', 6)])
        kb.op('dve', lambda e: e.tensor_copy(out=mT[:], in_=ps[:, 6, 0:384].rearrange("p (q c) -> p q c", q=3)),
              reads=[('ps', 6)], writes=['pk_mT'])
        for half in range(2):
            Pt, Qt = Poh[half], Qoh[half]
            pk_, qk_ = ('pk_P', half), ('pk_Q', half)
            for tt in range(64):
                t = half * 64 + tt
                kb.op('pool', lambda e: e.tensor_scalar(out=Pt[:, tt, :], in0=C['iota'][:], scalar1=mT[:, 0, t:t + 1],
                                                        scalar2=None, op0=ALU.is_equal),
                      reads=['pk_mT', 'C_iota'], writes=[pk_])
                kb.op('dve', lambda e: e.tensor_scalar(out=Qt[:, tt, :], in0=C['iota'][:], scalar1=mT[:, 1, t:t + 1],
                                                       scalar2=mT[:, 2, t:t + 1], op0=ALU.is_equal, op1=ALU.mult),
                      reads=['pk_mT', 'C_iota'], writes=[qk_])
            for g4 in range(16):
                b = 2 + g4 % 4
                for k in range(4):
                    tt = g4 * 4 + k
                    kb.op('pe', lambda e: e.matmul(ps[:, b, k * 128:(k + 1) * 128], lhsT=Qt[:, tt, :], rhs=Pt[:, tt, :],
                                                   start=True, stop=True), reads=[pk_, qk_], writes=[('ps', b)])
                t0 = half * 64 + g4 * 4
                dst = GT[:, :, t0:t0 + 4].rearrange("p i t -> p t i")
                evac(P, 'act' if g4 % 2 else 'dve', dst, ps[:, b, :].rearrange("p (t i) -> p t i", t=4), [('ps', b)], ['pk_GT'])
        for q in range(8):
            kb.op('sp', lambda e: e.dma_start(out=Gd[q * 16:(q + 1) * 16, :, it * 128:(it + 1) * 128].rearrange("i j t -> j i t"),
                                              in_=GT[:, q * 16:(q + 1) * 16, :]), reads=['pk_GT'], writes=['Gd'], dma=True)
    P.release(m0)


def peer_dense(P, X1d, Gd, uT_d, v_d, g_d, b_d, xout):
    kb, ps, C = P.kb, P.ps, P.C
    m0 = P.mark()
    z = P.sbuf("zacc", [128, NT, D], F32)
    m_in = P.mark()
    xT = P.sbuf("xTd", [128, 16, TOK], BF16)
    build_xT(P, X1d, xT, 'xTd')
    for i in range(NT):
        kb.op('sp', lambda e: e.dma_start(out=z[:, i, :], in_=X1d[i * 128:(i + 1) * 128, :]), reads=['X1d'], writes=[('z', i)], dma=True)
        kb.op('act', lambda e: e.mul(out=z[:, i, :], in_=z[:, i, :], mul=ALPHA), reads=[('z', i)], writes=[('z', i)])
    ub = [P.sbuf("pd_u%d" % i, [128, 16, 512], BF16) for i in range(2)]
    vb = [P.sbuf("pd_v%d" % i, [128, 4, D], BF16) for i in range(1)]
    gb = [P.sbuf("pd_g%d" % i, [128, 4, TOK], BF16) for i in range(2)]
    wT = P.sbuf("pd_w", [128, 4, TOK], BF16)
    hb = [P.sbuf("pd_h%d" % i, [128, 512], BF16) for i in range(2)]
    NG = 32
    for g in range(NG):
        u = ub[g % 2]
        uk = ('pd_u', g % 2)
        load_wblk(P, u, uk, uT_d, g * 512, 512, 16)
        v = vb[0]
        vk = ('pd_v', 0)
        kb.op('pool', lambda e: e.dma_start(out=v[:], in_=v_d[g * 512:(g + 1) * 512, :].rearrange("(i j) d -> j i d", j=128)),
              writes=[vk], dma=True)
        gg = gb[g % 2]
        gk = ('pd_g', g % 2)
        kb.op('sp', lambda e: e.dma_start(out=gg[:], in_=Gd[g * 4:(g + 1) * 4].rearrange("i j t -> j i t")),
              reads=['Gd'], writes=[gk], dma=True)
        n = 0
        for ii in range(4):
            for tb in range(2):
                b = 2 + n % 2
                for dc in range(16):
                    kb.op('pe', lambda e: e.matmul(ps[:, b, :], lhsT=u[:, dc, ii * 128:(ii + 1) * 128],
                                                   rhs=xT[:, dc, tb * 512:(tb + 1) * 512], start=(dc == 0), stop=(dc == 15)),
                          reads=[uk, 'xTd'], writes=[('ps', b)])
                h = hb[n % 2]
                hk = ('pd_h', n % 2)
                kb.op('act', lambda e: e.activation(out=h[:], in_=ps[:, b, :], func=AF.Gelu_apprx_tanh),
                      reads=[('ps', b)], writes=[hk])
                kb.op('pool', lambda e: e.tensor_tensor(out=wT[:, ii, tb * 512:(tb + 1) * 512], in0=h[:],
                                                        in1=gg[:, ii, tb * 512:(tb + 1) * 512], op=ALU.mult),
                      reads=[hk, gk], writes=['pd_w'])
                n += 1
        n = 0
        for i in range(NT):
            for cb in range(4):
                b = 4 + n % 4
                for ii in range(4):
                    kb.op('pe', lambda e: e.matmul(ps[:, b, :], lhsT=wT[:, ii, i * 128:(i + 1) * 128],
                                                   rhs=v[:, ii, cb * 512:(cb + 1) * 512], start=(ii == 0), stop=(ii == 3)),
                          reads=['pd_w', vk], writes=[('ps', b)])
                dst = z[:, i, cb * 512:(cb + 1) * 512]
                kb.op('dve', lambda e: e.tensor_tensor(out=dst, in0=dst, in1=ps[:, b, :], op=ALU.add),
                      reads=[('ps', b), ('z', i)], writes=[('z', i)])
                n += 1
    P.release(m_in)

    def store(i, x, k):
        kb.op('sp', lambda e: e.dma_start(out=xout[i * 128:(i + 1) * 128, :], in_=x), reads=[k], writes=['xout'], dma=True)
    layer_norm_tiles(P, z, lambda i: ('z', i), g_d, b_d, 1e-5, store)
    P.release(m0)


def c_inputs(P):
    d = {}
    d['x'] = P.dram_in("x", [TOK, D], F32)
    d['w_out'] = P.dram_in("w_out", [D, D], F32)
    d['g1'] = P.dram_in("ln1_g", [D], F32)
    d['b1'] = P.dram_in("ln1_b", [D], F32)
    d['g2'] = P.dram_in("ln2_g", [D], F32)
    d['b2'] = P.dram_in("ln2_b", [D], F32)
    d['wq'] = P.dram_in("peer_wq", [D, D], F32)
    d['skT'] = P.dram_in("peer_skT", [16, 128, 128], F32)
    d['uT'] = P.dram_in("peer_uT", [D, 16384], F32)
    d['v'] = P.dram_in("peer_v", [16384, D], F32)
    d['xout'] = P.dram_out("xout", [TOK, D], F32)
    d['X1d'] = P.dram_out("x1dbg", [TOK, D], F32)
    d["Gd"] = P.dram_out("Gd", [128, 128, TOK], BF16)
    return d


def stage_c(P, oT, d, oT_mark):
    stage_c1(P, oT, d['x'], d['w_out'], d['g1'], d['b1'], d['X1d'])
    P.release(oT_mark)
    peer_select(P, d['X1d'], d['wq'], d['skT'], d['Gd'])
    peer_dense(P, d['X1d'], d['Gd'], d['uT'], d['v'], d['g2'], d['b2'], d['xout'])


def build_c_test():
    P = Prog()
    P.load_consts(['ident', 'iota', 'iota16'])
    oTd = P.dram_in("oTd", [16, 128, TOK], BF16)
    d = c_inputs(P)
    mk = P.mark()
    oT = P.sbuf("oT_sb", [128, 16, TOK], BF16)
    P.kb.op('sp', lambda e: e.dma_start(out=oT[:], in_=oTd.rearrange("f p t -> p f t")), writes=['oT'], dma=True)
    stage_c(P, oT, d, mk)
    P.end()
    return P


def stage_a_odd(P, xsrc, w_in, gq_d, gkv_d, w_uq, w_ukv, rope_d, QT, KT, V):
    kb, ps, C = P.kb, P.ps, P.C
    m = P.mark()
    xT = P.sbuf("xT", [128, 16, TOK], BF16)
    build_xT(P, xsrc, xT, 'xT')
    wb = [P.sbuf("wb%d" % i, [128, 16, 512], BF16) for i in range(2)]
    st = Stager(P, "stg", 4)
    s64 = [P.sbuf("s64_%d" % i, [128, 512], BF16) for i in range(2)]
    for i in range(2):
        kb.op('dve', lambda e: e.memset(s64[i][:], 0.0), writes=[('s64', i)])
    rope = P.sbuf("rope_t", [128, 4, TOK], F32)
    kb.op('sp', lambda e: e.dma_start(out=rope[:], in_=rope_d.rearrange("k p t -> p k t")), writes=['rope_t'], dma=True)
    gq = P.sbuf("gq_t", [128, 4], F32)
    gkv = P.sbuf("gkv_t", [128, 4], F32)
    kb.op('sp', lambda e: e.dma_start(out=gq[:], in_=gq_d.rearrange("(c p) -> p c", p=128), allow_slow_non_contiguous=True), writes=['gq_t'], dma=True)
    kb.op('sp', lambda e: e.dma_start(out=gkv[:], in_=gkv_d.rearrange("(c p) -> p c", p=128), allow_slow_non_contiguous=True), writes=['gkv_t'], dma=True)
    cg = {'q': P.sbuf("cgq", [128, 4, TOK], BF16), 'kv': P.sbuf("cgkv", [128, 4, TOK], BF16)}
    sq = [P.sbuf("sqt%d" % i, [128, 512], BF16) for i in range(2)]
    rstd = {'q': P.sbuf("rstdq", [128, TOK], F32), 'kv': P.sbuf("rstdkv", [128, TOK], F32)}
    t1s = [P.sbuf("rp_a%d" % i, [128, 512], F32) for i in range(2)]
    t2s = [P.sbuf("rp_b%d" % i, [128, 512], F32) for i in range(2)]
    cnt = [0]
    rcnt = [0]

    def rope_and_store(xb, xbk, nrows, kind, tb, dst_ap, outt, outk):
        j = rcnt[0] % 2
        rcnt[0] += 1
        kb.op('pe', lambda e: e.matmul(ps[:, 6 + j, :], lhsT=C['rot'][0:nrows, kind, :], rhs=xb[0:nrows, :],
                                       start=True, stop=True), reads=[xbk, 'C_rot'], writes=[('ps', 6 + j)])
        cosap = rope[0:nrows, 2 * kind, tb * 512:(tb + 1) * 512]
        sinap = rope[0:nrows, 2 * kind + 1, tb * 512:(tb + 1) * 512]
        t1, t2 = t1s[j], t2s[j]
        kb.op('pool', lambda e: e.tensor_tensor(out=t1[0:nrows, :], in0=xb[0:nrows, :], in1=cosap, op=ALU.mult),
              reads=[xbk, 'rope_t'], writes=[('rp_a', j)])
        kb.op('dve', lambda e: e.tensor_tensor(out=t2[0:nrows, :], in0=ps[0:nrows, 6 + j, :], in1=sinap, op=ALU.mult),
              reads=[('ps', 6 + j), 'rope_t'], writes=[('rp_b', j)])
        kb.op('dve', lambda e: e.tensor_tensor(out=outt[0:nrows, :], in0=t1[0:nrows, :], in1=t2[0:nrows, :], op=ALU.add),
              reads=[('rp_a', j), ('rp_b', j)], writes=[outk])
        kb.op('sp', lambda e: e.dma_start(out=dst_ap, in_=outt[:, :]), reads=[outk], writes=[], dma=True)

    def mk_lat(name, ch, gt_):
        def sink(psap, tb, pskey):
            kb.op('act', lambda e: e.activation(out=cg[name][:, ch, tb * 512:(tb + 1) * 512], in_=psap, func=AF.Copy,
                                                scale=gt_[:, ch:ch + 1]), reads=[pskey, 'gq_t', 'gkv_t'], writes=['cg' + name])
            sqt = sq[cnt[0] % 2]
            sqk = ('sqt', cnt[0] % 2)
            cnt[0] += 1
            kb.op('act', lambda e: e.activation(out=sqt[:], in_=psap, func=AF.Square), reads=[pskey], writes=[sqk])
            b = 6 + tb
            kb.op('pe', lambda e: e.matmul(ps[:, b, :], lhsT=C['ones'][:], rhs=sqt[:], start=(ch == 0), stop=(ch == 3)),
                  reads=[sqk, 'C_ones'], writes=[('ps', b)])
            if ch == 3:
                r = rstd[name][:, tb * 512:(tb + 1) * 512]
                kb.op('dve', lambda e: e.tensor_scalar(out=r, in0=ps[:, b, :], scalar1=1.0 / 512, scalar2=1e-6,
                                                       op0=ALU.mult, op1=ALU.add), reads=[('ps', b)], writes=['rstd' + name])
                kb.op('act', lambda e: e.activation(out=r, in_=r, func=AF.Sqrt), reads=['rstd' + name], writes=['rstd' + name])
                kb.op('dve', lambda e: e.reciprocal(out=r, in_=r), reads=['rstd' + name], writes=['rstd' + name])
        return sink
    PH = os.environ.get('AO_PH', 'lat,rope,dv,kvs,upq,upkv,vm').split(',')
    if 'lat' in PH:
        proj_fm(P, xT, 'xT', 16, w_in, [(c * 128, 128) for c in range(4)], [mk_lat('q', c, gq) for c in range(4)], wb, st)
        proj_fm(P, xT, 'xT', 16, w_in, [(512 + c * 128, 128) for c in range(4)], [mk_lat('kv', c, gkv) for c in range(4)], wb, st)

    def mk_rope_sink(dst, ch, nrows, kind, scale):
        def sink(psap, tb, pskey):
            if nrows == 64:
                j = cnt[0] % 2
                xb, xbk = s64[j], ('s64', j)
            else:
                xb, xbk = st.next()
            cnt[0] += 1
            evac(P, 'act', xb[0:nrows, :], psap, [pskey], [xbk], scale)
            if nrows == 64:
                outt, outk = xb, xbk
            else:
                outt, outk = st.next()
            rope_and_store(xb, xbk, nrows, kind, tb, dst[ch, :, tb * 512:(tb + 1) * 512], outt, outk)
        return sink
    cols = [(1024, 64)]
    sinks = [mk_rope_sink(KT, 8, 64, 1, None)]
    for ch in range(8):
        cols.append((1088 + ch * 128, 128)); sinks.append(mk_rope_sink(QT, 16 + ch, 128, 0, 128.0 ** -0.5))
    for ch in range(8):
        cols.append((2112 + ch * 128, 128)); sinks.append(mk_rope_sink(KT, 9 + ch, 128, 0, None))
    if 'rope' in PH:
        proj_fm(P, xT, 'xT', 16, w_in, cols, sinks, wb, st)

    def dvsink(psap, i, cb, n, pskey):
        t, tk = st.next()
        evac(P, 'act' if cb % 2 else 'dve', t[:, 0:n], psap, [pskey], [tk])
        kb.op('sp', lambda e: e.dma_start(out=V[i * 128:(i + 1) * 128, 1024 + cb * 512:1024 + cb * 512 + n], in_=t[:, 0:n]),
              reads=[tk], writes=[], dma=True)
    if 'dv' in PH:
        proj_tm(P, xT, 'xT', 16, w_in, 3136, 1024, dvsink, wb)

    rcol = P.sbuf("rcol", [128, NT, 4], F32)
    tmpf = P.sbuf("tmpf", [128, 512], F32)

    def kvsink(psap, i, cb, n, pskey):
        kb.op('act', lambda e: e.activation(out=tmpf[:], in_=psap, func=AF.Square), reads=[pskey], writes=['tmpf'])
        kb.op('dve', lambda e: e.reduce_sum(out=rcol[:, i, 0:1], in_=tmpf[:], axis=AX.X), reads=['tmpf'], writes=['rcol'])
        kb.op('dve', lambda e: e.tensor_scalar(out=rcol[:, i, 1:2], in0=rcol[:, i, 0:1], scalar1=1.0 / 512, scalar2=1e-6,
                                               op0=ALU.mult, op1=ALU.add), reads=['rcol'], writes=['rcol'])
        kb.op('act', lambda e: e.activation(out=rcol[:, i, 2:3], in_=rcol[:, i, 1:2], func=AF.Sqrt), reads=['rcol'], writes=['rcol'])
        kb.op('dve', lambda e: e.reciprocal(out=rcol[:, i, 3:4], in_=rcol[:, i, 2:3]), reads=['rcol'], writes=['rcol'])
    if 'kvs' in PH:
        proj_tm(P, xT, 'xT', 16, w_in, 512, 512, kvsink, wb)

    sc_m = 192.0 ** -0.5

    def mk_up_sink(name, dst, ch, nrows, scale, ropekind):
        def sink(psap, tb, pskey):
            r = rstd[name][0:nrows, tb * 512:(tb + 1) * 512]
            if nrows == 64:
                j = cnt[0] % 2
                xb, xbk = s64[j], ('s64', j)
            else:
                xb, xbk = st.next()
            cnt[0] += 1
            kb.op('dve', lambda e: e.scalar_tensor_tensor(out=xb[0:nrows, :], in0=psap, scalar=float(scale), in1=r,
                                                          op0=ALU.mult, op1=ALU.mult), reads=[pskey, 'rstd' + name], writes=[xbk])
            if ropekind is None:
                kb.op('sp', lambda e: e.dma_start(out=dst[ch, :, tb * 512:(tb + 1) * 512], in_=xb[:, :]), reads=[xbk], dma=True)
            else:
                rope_and_store(xb, xbk, nrows, ropekind, tb, dst[ch, :, tb * 512:(tb + 1) * 512], xb, xbk)
        return sink
    cols, sinks = [], []
    for h in range(8):
        cols.append((h * 192, 128)); sinks.append(mk_up_sink('q', QT, h, 128, sc_m, None))
    for h in range(8):
        cols.append((h * 192 + 128, 64)); sinks.append(mk_up_sink('q', QT, 8 + h, 64, sc_m, 1))
    if 'upq' in PH:
        proj_fm(P, cg['q'], 'cgq', 4, w_uq, cols, sinks, wb, st)
    cols, sinks = [], []
    for h in range(8):
        cols.append((h * 256, 128)); sinks.append(mk_up_sink('kv', KT, h, 128, 1.0, None))
    if 'upkv' in PH:
        proj_fm(P, cg['kv'], 'cgkv', 4, w_ukv, cols, sinks, wb, st)

    def gsel(wblk, wkey, cb):
        for hh in range(4):
            c0 = (cb * 4 + hh) * 256 + 128
            kb.op('pool', lambda e: e.dma_start(out=wblk[:, 0:4, hh * 128:(hh + 1) * 128],
                                                in_=w_ukv[:, c0:c0 + 128].rearrange("(dc p) c -> p dc c", p=128)),
                  writes=[wkey], dma=True)

    def vmsink(psap, i, cb, n, pskey):
        t, tk = st.next()
        kb.op('act', lambda e: e.activation(out=t[:, 0:n], in_=psap, func=AF.Copy, scale=rcol[:, i, 3:4]),
              reads=[pskey, 'rcol'], writes=[tk])
        kb.op('sp', lambda e: e.dma_start(out=V[i * 128:(i + 1) * 128, cb * 512:cb * 512 + n], in_=t[:, 0:n]),
              reads=[tk], writes=[], dma=True)
    if 'vm' in PH:
        proj_tm(P, cg['kv'], 'cgkv', 4, w_ukv, 0, 1024, vmsink, wb, gsel=gsel)
    P.release(m)


def build_a_odd():
    P = Prog()
    P.load_consts(['ident', 'ones', 'rot'])
    xsrc = P.dram_in("x", [TOK, D], F32)
    w_in = P.dram_in("w_in", [D, 4160], F32)
    gq = P.dram_in("g_q", [512], F32)
    gkv = P.dram_in("g_kv", [512], F32)
    w_uq = P.dram_in("w_uq", [512, 1536], F32)
    w_ukv = P.dram_in("w_ukv", [512, 2048], F32)
    rope_d = P.dram_in("rope", [4, 128, TOK], F32)
    QT = P.dram_out("QT", [24, 128, TOK], BF16)
    KT = P.dram_out("KT", [17, 128, TOK], BF16)
    V = P.dram_out("V", [TOK, 2048], BF16)
    stage_a_odd(P, xsrc, w_in, gq, gkv, w_uq, w_ukv, rope_d, QT, KT, V)
    P.end()
    return P


def odd_units():
    us = []
    for h in range(8):
        us.append(dict(q=[h, 8 + h], k=[h, 8], v0=h * 128, dv=128, mode='chunk', h=h, fc=h, kind='mla'))
    for hd in range(4):
        for c in range(2):
            us.append(dict(q=[16 + hd * 2 + c], k=[9 + hd * 2 + c], v0=1024 + hd * 256, dv=256, mode='chunk', h=hd,
                           fc=8 + hd * 2, kind='diff', comp=c))
    return us


def stage_b_odd(P, QT, Kown, Koth, Vown, Voth, visb_d, dl_d, gsub_d, lamc_d, oT):
    kb, ps, C = P.kb, P.ps, P.C
    visb = P.sbuf("visb", [128, 24], F32)
    kb.op('sp', lambda e: e.dma_start(out=visb[:], in_=visb_d), writes=['visb'], dma=True)
    dl = P.sbuf("dl", [128, 4, 128], F32)
    lamc = P.sbuf("lamc", [128, 2], F32)
    gs2 = P.sbuf("gs2", [128, 2], F32)
    lw = P.sbuf("lamw", [128, 8], F32)
    pr = P.sbuf("lampr", [128, 2, 128], F32)
    kb.op('sp', lambda e: e.dma_start(out=dl[:].rearrange("p a b -> p (a b)"),
                                      in_=dl_d.rearrange("a b -> (a b)").partition_broadcast(128)), writes=['dl'], dma=True)
    kb.op('sp', lambda e: e.dma_start(out=lamc[:], in_=lamc_d.partition_broadcast(128)), writes=['lamc'], dma=True)
    kb.op('sp', lambda e: e.dma_start(out=gs2[:], in_=gsub_d.rearrange("(c p) -> p c", p=128), allow_slow_non_contiguous=True), writes=['gs2'], dma=True)
    kb.op('dve', lambda e: e.tensor_tensor(out=pr[:, 0, :], in0=dl[:, 0, :], in1=dl[:, 1, :], op=ALU.mult), reads=['dl'], writes=['lampr'])
    kb.op('dve', lambda e: e.tensor_tensor(out=pr[:, 1, :], in0=dl[:, 2, :], in1=dl[:, 3, :], op=ALU.mult), reads=['dl'], writes=['lampr'])
    kb.op('dve', lambda e: e.reduce_sum(out=lw[:, 0:2], in_=pr[:], axis=AX.X), reads=['lampr'], writes=['lamw'])
    kb.op('act', lambda e: e.activation(out=lw[:, 2:4], in_=lw[:, 0:2], func=AF.Exp), reads=['lamw'], writes=['lamw'])
    kb.op('dve', lambda e: e.tensor_tensor(out=lw[:, 4:5], in0=lw[:, 2:3], in1=lw[:, 3:4], op=ALU.subtract), reads=['lamw'], writes=['lamw'])
    kb.op('dve', lambda e: e.tensor_tensor(out=lw[:, 5:6], in0=lw[:, 4:5], in1=lamc[:, 0:1], op=ALU.add), reads=['lamw', 'lamc'], writes=['lamw'])
    kb.op('dve', lambda e: e.tensor_scalar(out=lw[:, 6:7], in0=lw[:, 5:6], scalar1=-1.0, scalar2=None, op0=ALU.mult),
          reads=['lamw'], writes=['lamw'])
    kb.op('dve', lambda e: e.tensor_scalar(out=gs2[:], in0=gs2[:], scalar1=lamc[:, 1:2], scalar2=None, op0=ALU.mult),
          reads=['gs2', 'lamc'], writes=['gs2'])
    A1 = P.sbuf("dfA1", [128, 2, TOK], F32)
    a2n = P.sbuf("dfa2", [128, 128], F32)
    dd_ = P.sbuf("dfd", [128, 2, 128], F32)
    sqd = P.sbuf("dfsq", [128, 2, 128], BF16)
    rs = P.sbuf("dfrs", [128, 128], F32)
    units = odd_units()

    def sink(ui, qt, c, O, rinv, keys, get_s):
        u = units[ui]
        cols = slice(qt * 128, (qt + 1) * 128)
        if u['kind'] == 'mla':
            kb.op('dve', lambda e: e.tensor_tensor(out=oT[:, u['fc'], cols], in0=O, in1=rinv[:], op=ALU.mult),
                  reads=keys, writes=['oT'])
            return
        if u['comp'] == 0:
            kb.op('dve', lambda e: e.tensor_tensor(out=A1[:, c, cols], in0=O, in1=rinv[:], op=ALU.mult),
                  reads=keys, writes=['dfA1'])
            return
        kb.op('dve', lambda e: e.tensor_tensor(out=a2n[:], in0=O, in1=rinv[:], op=ALU.mult), reads=keys, writes=['dfa2'])
        kb.op('dve', lambda e: e.scalar_tensor_tensor(out=dd_[:, c, :], in0=a2n[:], scalar=lw[:, 6:7], in1=A1[:, c, cols],
                                                      op0=ALU.mult, op1=ALU.add), reads=['dfa2', 'lamw', 'dfA1'], writes=['dfd'])
        kb.op('pool', lambda e: e.tensor_tensor(out=sqd[:, c, :], in0=dd_[:, c, :], in1=dd_[:, c, :], op=ALU.mult),
              reads=['dfd'], writes=['dfsq'])
        if c == 1:
            S, sk = get_s()
            for c2 in range(2):
                kb.op('pe', lambda e: e.matmul(S, lhsT=C['ones'][:], rhs=sqd[:, c2, :], start=(c2 == 0), stop=(c2 == 1)),
                      reads=['dfsq', 'C_ones'], writes=[sk])
            kb.op('dve', lambda e: e.tensor_scalar(out=rs[:], in0=S, scalar1=1.0 / 256, scalar2=1e-6, op0=ALU.mult, op1=ALU.add),
                  reads=[sk], writes=['dfrs'])
            kb.op('act', lambda e: e.activation(out=rs[:], in_=rs[:], func=AF.Sqrt), reads=['dfrs'], writes=['dfrs'])
            kb.op('dve', lambda e: e.reciprocal(out=rs[:], in_=rs[:]), reads=['dfrs'], writes=['dfrs'])
            for c2 in range(2):
                kb.op('dve', lambda e: e.scalar_tensor_tensor(out=oT[:, u['fc'] + c2, cols], in0=dd_[:, c2, :],
                                                              scalar=gs2[:, c2:c2 + 1], in1=rs[:], op0=ALU.mult, op1=ALU.mult),
                      reads=['dfd', 'gs2', 'dfrs'], writes=['oT'])
    attention(P, units, QT, Kown, Koth, Vown, Voth, visb, sink, nb=None, bmat=None, sink_gets=True)

def build_bc(kind):
    even = (kind == 'even')
    P = Prog()
    P.load_consts(['ident', 'ones', 'masks', 'iota', 'iota16'] + (['trib', 'supb', 'sel63b'] if even else []))
    NQ, NK = (16, 16) if even else (24, 17)
    QT = P.dram_in("QT", [NQ, 128, TOK], BF16)
    Kown = P.dram_in("Kown", [NK, 128, TOK], BF16)
    Koth = P.dram_in("Koth", [NK, 128, 3072], BF16)
    Vown = P.dram_in("Vown", [TOK, 2048], BF16)
    Voth = P.dram_in("Voth", [3072, 2048], BF16)
    visb = P.dram_in("visb", [128, 24], F32)
    if even:
        LFown = P.dram_in("LFown", [TOK, 8], F32)
        LFoth = P.dram_in("LFoth", [3072, 8], F32)
        biasd = P.dram_in("biasd", [8, 3, 128, 128], F32)
    else:
        dl = P.dram_in("dlam", [4, 128], F32)
        gsub = P.dram_in("gsub", [256], F32)
        lamc = P.dram_in("lamc", [2], F32)
    d = c_inputs(P)
    mk = P.mark()
    oT = P.sbuf("oT_sb", [128, 16, TOK], BF16)
    if even:
        stage_b_even(P, QT, Kown, Koth, Vown, Voth, LFown, LFoth, visb, biasd, oT)
    else:
        stage_b_odd(P, QT, Kown, Koth, Vown, Voth, visb, dl, gsub, lamc, oT)
    stage_c(P, oT, d, mk)
    P.end()
    return P


_PROGS = {}


def get_prog(name):
    if name not in _PROGS:
        _PROGS[name] = {'ae': build_a_even, 'ao': build_a_odd, 'be': lambda: build_bc('even'),
                        'bo': lambda: build_bc('odd')}[name]()
    return _PROGS[name]


def run_layer(l, xs, inp):
    i = l // 2
    even = (l % 2 == 0)
    f32 = lambda a: np.ascontiguousarray(np.asarray(a, np.float32))
    PA = get_prog('ae' if even else 'ao')
    maps = []
    for c in range(8):
        r = c % 4
        if even:
            m = {"x": xs[c], "w_in": f32(inp['w_in_even'][i]), "bfg": f32(inp['b_forget'][i])}
        else:
            m = {"x": xs[c], "w_in": f32(inp['w_in_odd'][i]), "g_q": f32(inp['g_q_lora'][i]), "g_kv": f32(inp['g_kv_lora'][i]),
                 "w_uq": f32(inp['w_uq'][i]), "w_ukv": f32(inp['w_ukv'][i]),
                 "rope": rope_tables(np.arange(r * 1024, (r + 1) * 1024))}
        m.update(PA.hc)
        maps.append(m)
    ra = run_bass_kernel_spmd(PA.nc, maps, core_ids=list(range(8))).results
    PB = get_prog('be' if even else 'bo')
    uT = np.ascontiguousarray(np.asarray(inp['peer_u'][l], np.float32).T)
    skT = np.ascontiguousarray(np.asarray(inp['peer_sub_keys'][l], np.float32).reshape(16, 128, 128).transpose(0, 2, 1))
    common = dict(w_out=f32(inp['w_out_even' if even else 'w_out_odd'][i]), ln1_g=f32(inp['ln_mix_g'][l]),
                  ln1_b=f32(inp['ln_mix_b'][l]), ln2_g=f32(inp['ln_ffn_g'][l]), ln2_b=f32(inp['ln_ffn_b'][l]),
                  peer_wq=f32(inp['peer_w_query'][l]), peer_skT=skT, peer_uT=uT, peer_v=f32(inp['peer_v'][l]))
    if even:
        common['biasd'] = band_bias_mats(f32(inp['rel_bias'][i]))
    else:
        lam_init = 0.8 - 0.6 * math.exp(-0.3 * l)
        common['dlam'] = f32(inp['diff_lambda'][i])
        common['gsub'] = f32(inp['g_subln'][i])
        common['lamc'] = np.array([lam_init, 1.0 - lam_init], np.float32)
    maps = []
    for c in range(8):
        b, r = c // 4, c % 4
        oo, nvis = others_order(r)
        grp = [ra[b * 4 + rr] for rr in range(4)]

        def tile_of(name, mtile, fm):
            src = grp[mtile // 8][name]
            k = mtile % 8
            return src[:, :, k * 128:(k + 1) * 128] if fm else src[k * 128:(k + 1) * 128]
        visb = np.zeros((128, 24), np.float32)
        visb[:, nvis:] = NEG
        m = dict(QT=ra[c]["QT"], Kown=ra[c]["KT"],
                 Koth=np.ascontiguousarray(np.concatenate([tile_of("KT", mt, True) for mt in oo], axis=2)),
                 Vown=ra[c]["V"],
                 Voth=np.ascontiguousarray(np.concatenate([tile_of("V", mt, False) for mt in oo], axis=0)),
                 visb=visb, x=xs[c])
        if even:
            m['LFown'] = ra[c]["LF"]
            m['LFoth'] = np.ascontiguousarray(np.concatenate([tile_of("LF", mt, False) for mt in oo], axis=0))
        m.update(common)
        m.update(PB.hc)
        maps.append(m)
    rb = run_bass_kernel_spmd(PB.nc, maps, core_ids=list(range(8))).results
    return [np.ascontiguousarray(rb[c]["xout"]) for c in range(8)], ra, rb


def kernel(**inputs):
    x = np.asarray(inputs['x'], np.float32)
    xs = [np.ascontiguousarray(x[c // 4, (c % 4) * 1024:(c % 4 + 1) * 1024]) for c in range(8)]
    for l in range(DEPTH):
        xs, _, _ = run_layer(l, xs, inputs)
    out = np.empty((2, 4096, D), np.float32)
    for c in range(8):
        out[c // 4, (c % 4) * 1024:(c % 4 + 1) * 1024] = xs[c]
    return out
```
